# Optimizing a Trainium2 kernel written in Bass

```python
import jax, jax.numpy as jnp
from jax import lax
import numpy as np

D_MODEL = 4096
BATCH = 1
SEQ = 16384
DEPTH = 2

N_META = 16
MIX_W = D_MODEL
M_HEADS = 4
M_W = MIX_W // 2
M_V = M_W // M_HEADS
M_QK = M_V // 2
M_CHUNK = 64
CONV_W = 4
A_HEADS = 16
A_W = MIX_W - M_W
A_V = A_W // A_HEADS
NOPE = 128
ROPE = 64
Q_LORA = 1536
KV_LORA = 512
Q_BLOCK = 128
ROPE_THETA = 10000.0
NORM_EPS = 1e-6
D_FF = -(-8 * D_MODEL // (3 * 256)) * 256
IN_SIZES = (M_HEADS * M_QK, M_HEADS * M_QK, M_W, M_W, M_HEADS, M_HEADS, Q_LORA, KV_LORA, ROPE)
N_IN = sum(IN_SIZES)
NEG_SCORE = -1e30

kernel_name = 'hybrid_mlstm_mla_meta_block'


def rmsnorm(x, g):
    xf = x.astype(jnp.float32)
    y = xf * lax.rsqrt(jnp.mean(xf * xf, axis=-1, keepdims=True) + NORM_EPS)
    return (y * g.astype(jnp.float32)).astype(x.dtype)


def split_cols(z):
    idx, acc = [], 0
    for s in IN_SIZES[:-1]:
        acc += s
        idx.append(acc)
    return jnp.split(z, idx, axis=-1)


def apply_rope(x, cos, sin):
    half = x.shape[-1] // 2
    x1 = x[..., :half].astype(jnp.float32)
    x2 = x[..., half:].astype(jnp.float32)
    return jnp.concatenate([x1 * cos - x2 * sin, x1 * sin + x2 * cos], axis=-1).astype(x.dtype)


def causal_dwconv(x, w):
    k_w, c = w.shape
    return lax.conv_general_dilated(x, w[:, None, :].astype(x.dtype), window_strides=(1,),
                                    padding=((k_w - 1, 0),), dimension_numbers=('NWC', 'WIO', 'NWC'),
                                    feature_group_count=c)


def mlstm_chunkwise(q, k, v, log_i, log_f):
    B, H, Lp, dk = q.shape
    dv = v.shape[-1]
    nc = Lp // M_CHUNK

    def chunks(t):
        return jnp.moveaxis(t.reshape(B, H, nc, M_CHUNK, *t.shape[3:]), 2, 0)

    tril = jnp.tril(jnp.ones((M_CHUNK, M_CHUNK), dtype=bool))

    def step(carry, xs):
        C, n, m = carry
        qc, kc, vc, li, lf = xs
        qf = qc.astype(jnp.float32)
        kf = kc.astype(jnp.float32)
        vf = vc.astype(jnp.float32)
        b = jnp.cumsum(lf, axis=-1)
        g = b[..., -1]
        d = jnp.where(tril, b[..., :, None] - b[..., None, :] + li[..., None, :], -jnp.inf)
        inter = b + m[..., None]
        m_t = jnp.maximum(inter, jnp.max(d, axis=-1))
        w_inter = jnp.exp(inter - m_t)
        s = jnp.einsum('bhtd,bhsd->bhts', qf, kf) * jnp.exp(d - m_t[..., None])
        num = w_inter[..., None] * jnp.einsum('bhtd,bhde->bhte', qf, C) + jnp.einsum('bhts,bhse->bhte', s, vf)
        den = w_inter * jnp.einsum('bhtd,bhd->bht', qf, n) + jnp.sum(s, axis=-1)
        h = num / jnp.maximum(jnp.abs(den), jnp.exp(-m_t))[..., None]
        a = g[..., None] - b + li
        m_new = jnp.maximum(g + m, jnp.max(a, axis=-1))
        decay = jnp.exp(g + m - m_new)
        wk = kf * jnp.exp(a - m_new[..., None])[..., None]
        C_new = decay[..., None, None] * C + jnp.einsum('bhsd,bhse->bhde', wk, vf)
        n_new = decay[..., None] * n + jnp.sum(wk, axis=2)
        return (C_new, n_new, m_new), h

    init = (jnp.zeros((B, H, dk, dv), jnp.float32), jnp.zeros((B, H, dk), jnp.float32),
            jnp.zeros((B, H), jnp.float32))
    xs = (chunks(q), chunks(k), chunks(v), chunks(log_i), chunks(log_f))
    _, h = lax.scan(step, init, xs)
    return jnp.moveaxis(h, 0, 2).reshape(B, H, Lp, dv)


def mlstm_group(q_in, k_in, v_in, o_pre, i_pre, f_pre, conv_w, b_gates, g_mnorm):
    B, L, _ = v_in.shape
    qk = jax.nn.silu(causal_dwconv(jnp.concatenate([q_in, k_in], axis=-1), conv_w))
    q, k = jnp.split(qk, 2, axis=-1)
    pad = (-L) % M_CHUNK

    def heads(t, dim):
        t = t.reshape(B, L, M_HEADS, dim).transpose(0, 2, 1, 3)
        return jnp.pad(t, ((0, 0), (0, 0), (pad, 0), (0, 0)))

    def gate_pad(t, fill):
        return jnp.pad(jnp.transpose(t, (0, 2, 1)), ((0, 0), (0, 0), (pad, 0)), constant_values=fill)

    log_i = gate_pad(i_pre.astype(jnp.float32) + b_gates[:M_HEADS].astype(jnp.float32), -jnp.inf)
    log_f = gate_pad(jax.nn.log_sigmoid(f_pre.astype(jnp.float32) + b_gates[M_HEADS:].astype(jnp.float32)), 0.0)
    h = mlstm_chunkwise(heads(q, M_QK) * (M_QK ** -0.5), heads(k, M_QK), heads(v_in, M_V), log_i, log_f)
    h = h[:, :, pad:].transpose(0, 2, 1, 3)
    h = rmsnorm(h, g_mnorm.reshape(M_HEADS, M_V))
    return (jax.nn.sigmoid(o_pre.astype(jnp.float32)) * h.reshape(B, L, M_W)).astype(v_in.dtype)


def mla_group(c_q, c_kv, k_rope, g_cq, w_uq, g_ckv, w_ukv, cos, sin):
    B, L, _ = c_q.shape
    q = (rmsnorm(c_q, g_cq) @ w_uq).reshape(B, L, A_HEADS, NOPE + ROPE)
    q = jnp.concatenate([q[..., :NOPE], apply_rope(q[..., NOPE:], cos[:, None, :], sin[:, None, :])], axis=-1)
    kv = (rmsnorm(c_kv, g_ckv) @ w_ukv).reshape(B, L, A_HEADS, NOPE + A_V)
    kr = apply_rope(k_rope, cos, sin)
    k = jnp.concatenate([kv[..., :NOPE], jnp.broadcast_to(kr[:, :, None, :], (B, L, A_HEADS, ROPE))], axis=-1)
    v = kv[..., NOPE:]
    pad = (-L) % Q_BLOCK
    Lp = L + pad
    nb = Lp // Q_BLOCK

    def prep(t):
        return jnp.pad(t, ((0, 0), (pad, 0), (0, 0), (0, 0))).transpose(0, 2, 1, 3)

    q, k, v = prep(q), prep(k), prep(v)
    q_blocks = jnp.moveaxis(q.reshape(B, A_HEADS, nb, Q_BLOCK, NOPE + ROPE), 2, 0)
    key_pos = jnp.arange(Lp)
    key_ok = key_pos >= pad
    scale = (NOPE + ROPE) ** -0.5

    def attend(args):
        qb, start = args
        s = jnp.einsum('bhqd,bhkd->bhqk', qb, k, preferred_element_type=jnp.float32) * scale
        q_pos = start + jnp.arange(Q_BLOCK)
        mask = (key_pos[None, :] <= q_pos[:, None]) & key_ok[None, :]
        p = jax.nn.softmax(jnp.where(mask, s, NEG_SCORE), axis=-1)
        return jnp.einsum('bhqk,bhkd->bhqd', p.astype(v.dtype), v)

    o = lax.map(attend, (q_blocks, jnp.arange(nb) * Q_BLOCK))
    o = jnp.moveaxis(o, 0, 2).reshape(B, A_HEADS, Lp, A_V)[:, :, pad:]
    return o.transpose(0, 2, 1, 3).reshape(B, L, A_W)


def hybrid_layer(x, cos, sin, g_mix_pre, w_in, conv_w, b_gates, g_mnorm, g_cq, w_uq, g_ckv, w_ukv,
                 w_out, g_mix_post, g_ffn_pre, w_gu, w_down, g_ffn_post):
    u = rmsnorm(x, g_mix_pre)
    mq, mk, mv, mo, mi, mf, cq, ckv, kr = split_cols(u @ w_in)
    h_m = mlstm_group(mq, mk, mv, mo, mi, mf, conv_w, b_gates, g_mnorm)
    h_a = mla_group(cq, ckv, kr, g_cq, w_uq, g_ckv, w_ukv, cos, sin)
    mix = jnp.concatenate([h_m, h_a.astype(h_m.dtype)], axis=-1) @ w_out
    x = x + rmsnorm(mix, g_mix_post)
    gate, up = jnp.split(rmsnorm(x, g_ffn_pre) @ w_gu, 2, axis=-1)
    y = (jax.nn.silu(gate) * up) @ w_down
    return x + rmsnorm(y, g_ffn_post)


def setup_inputs(seed: int = 0) -> dict:
    key = jax.random.key(seed)
    ks = jax.random.split(key, 20)
    f32 = jnp.float32

    def nrm(k, shape, scale):
        return jax.random.normal(k, shape, f32) * scale

    def gain(k, shape):
        return 1.0 + 0.05 * jax.random.normal(k, shape, f32)

    f_bias = jnp.linspace(3.0, 6.0, M_HEADS, dtype=f32)[None, :] + 0.1 * jax.random.normal(ks[8], (DEPTH, M_HEADS), f32)
    i_bias = 0.1 * jax.random.normal(ks[9], (DEPTH, M_HEADS), f32)
    return {
        'x': nrm(ks[0], (BATCH, SEQ, D_MODEL), 1.0),
        'meta': nrm(ks[1], (N_META, D_MODEL), 1.0),
        'g_mix_pre': gain(ks[2], (DEPTH, D_MODEL)),
        'w_in': nrm(ks[3], (DEPTH, D_MODEL, N_IN), D_MODEL ** -0.5),
        'conv_w': nrm(ks[4], (DEPTH, CONV_W, 2 * M_HEADS * M_QK), CONV_W ** -0.5),
        'b_gates': jnp.concatenate([i_bias, f_bias], axis=-1),
        'g_mnorm': gain(ks[5], (DEPTH, M_W)),
        'g_cq': gain(ks[6], (DEPTH, Q_LORA)),
        'w_uq': nrm(ks[7], (DEPTH, Q_LORA, A_HEADS * (NOPE + ROPE)), Q_LORA ** -0.5),
        'g_ckv': gain(ks[10], (DEPTH, KV_LORA)),
        'w_ukv': nrm(ks[11], (DEPTH, KV_LORA, A_HEADS * (NOPE + A_V)), KV_LORA ** -0.5),
        'w_out': nrm(ks[12], (DEPTH, MIX_W, D_MODEL), MIX_W ** -0.5),
        'g_mix_post': gain(ks[13], (DEPTH, D_MODEL)),
        'g_ffn_pre': gain(ks[14], (DEPTH, D_MODEL)),
        'w_gu': nrm(ks[15], (DEPTH, D_MODEL, 2 * D_FF), D_MODEL ** -0.5),
        'w_down': nrm(ks[16], (DEPTH, D_FF, D_MODEL), D_FF ** -0.5),
        'g_ffn_post': gain(ks[17], (DEPTH, D_MODEL)),
    }


def reference(x, meta, g_mix_pre, w_in, conv_w, b_gates, g_mnorm, g_cq, w_uq, g_ckv, w_ukv,
              w_out, g_mix_post, g_ffn_pre, w_gu, w_down, g_ffn_post):
    B = x.shape[0]
    h = jnp.concatenate([jnp.broadcast_to(meta[None].astype(x.dtype), (B, N_META, D_MODEL)), x], axis=1)
    L = h.shape[1]
    pos = jnp.arange(L, dtype=jnp.float32)
    inv_freq = ROPE_THETA ** (-jnp.arange(ROPE // 2, dtype=jnp.float32) / (ROPE // 2))
    ang = pos[:, None] * inv_freq[None, :]
    cos, sin = jnp.cos(ang), jnp.sin(ang)
    for l in range(DEPTH):
        h = hybrid_layer(h, cos, sin, g_mix_pre[l], w_in[l], conv_w[l], b_gates[l], g_mnorm[l], g_cq[l],
                         w_uq[l], g_ckv[l], w_ukv[l], w_out[l], g_mix_post[l], g_ffn_pre[l], w_gu[l],
                         w_down[l], g_ffn_post[l])
    return h[:, N_META:]
```

```python
import contextlib
import math
import numpy as np
import ml_dtypes
import concourse.bass as bass
import concourse.mybir as mybir
from concourse.bass_utils import run_bass_kernel_spmd

F32 = mybir.dt.float32
BF16 = mybir.dt.bfloat16
ALU = mybir.AluOpType
AF = mybir.ActivationFunctionType
NPBF = ml_dtypes.bfloat16

NCORES = 8
D = 4096
KD = D // 128
NMETA = 16
EPS = 1e-6


class Cfg:
    def __init__(self, SEQ=16384, DFF=11008, DEPTH=2):
        self.SEQ, self.DFF, self.DEPTH = SEQ, DFF, DEPTH
        self.Tc = NMETA + SEQ // NCORES
        self.Ttot = NMETA + SEQ


class Res:
    __slots__ = ("w", "rd")

    def __init__(self):
        self.w = None
        self.rd = []


class Prog:
    ENGS = ("pe", "act", "dve", "pool", "sp")

    def __init__(self, nc):
        self.nc = nc
        self.ops = {e: [] for e in self.ENGS}
        self.cnt = {e: 0 for e in self.ENGS}
        self.seen = {e: {} for e in self.ENGS}
        self.pend = {e: ([], []) for e in self.ENGS}
        self.dma_cnt = {}
        self.semkeys = ["E:" + e for e in self.ENGS]

    def op(self, eng, fn, reads=(), writes=(), inc=True, dma=None):
        waits = {}
        own = "E:" + eng

        def need(ev):
            if ev is None:
                return
            k, v = ev
            if k == own and eng == "pe":
                return
            if waits.get(k, 0) < v:
                waits[k] = v

        for r in reads:
            need(r.w)
        for w in writes:
            need(w.w)
            for ev in w.rd:
                need(ev)
        seen = self.seen[eng]
        wl = []
        for k, v in waits.items():
            if seen.get(k, 0) >= v:
                continue
            seen[k] = v
            wl.append((k, v))
        ev = None
        incspec = None
        if dma is not None:
            k = "D:" + dma
            if k not in self.dma_cnt:
                self.dma_cnt[k] = 0
                self.semkeys.append(k)
            self.dma_cnt[k] += 16
            ev = (k, self.dma_cnt[k])
            incspec = (k, 16)
            rds, wrs = list(reads), list(writes)
        elif inc:
            self.cnt[eng] += 1
            ev = (own, self.cnt[eng])
            incspec = (own, 1)
            pr, pw = self.pend[eng]
            rds, wrs = list(reads) + pr, list(writes) + pw
            self.pend[eng] = ([], [])
        else:
            pr, pw = self.pend[eng]
            pr.extend(reads)
            pw.extend(writes)
        self.ops[eng].append((fn, wl, incspec))
        if ev is not None:
            for r in rds:
                r.rd.append(ev)
            for w in wrs:
                w.w = ev
                w.rd = []
        return ev

    def emit(self):
        nc = self.nc
        for e in self.ENGS:
            assert not self.pend[e][0] and not self.pend[e][1], "pending non-inc ops on " + e
        with contextlib.ExitStack() as st:
            sems = {}
            for k in self.semkeys:
                sems[k] = st.enter_context(nc.semaphore(k.replace(":", "_")))
            block = st.enter_context(nc.Block())

            def run(ename):
                def body(eng):
                    for fn, wl, incspec in self.ops[ename]:
                        for k, v in wl:
                            eng.wait_ge(sems[k], v)
                        ins = fn(eng)
                        if incspec is not None:
                            ins.then_inc(sems[incspec[0]], incspec[1])
                    if ename == "sp":
                        for k, v in self.dma_cnt.items():
                            eng.wait_ge(sems[k], v)
                        for e in self.ENGS:
                            if self.cnt[e] > 0:
                                eng.wait_ge(sems["E:" + e], self.cnt[e])
                return body

            block.tensor(run("pe"))
            block.scalar(run("act"))
            block.vector(run("dve"))
            block.gpsimd(run("pool"))
            block.sync(run("sp"))


class Tl:
    def __init__(self, t):
        self.t = t
        self.r = Res()

    def __getitem__(self, idx):
        return self.t[idx]


class Ctx:
    _uid = [0]

    def __init__(self, nc):
        self.nc = nc
        self.st = contextlib.ExitStack()
        self.P = Prog(nc)
        Ctx._uid[0] += 1000
        self.n = Ctx._uid[0]

    def sb(self, shape, dt, name=None):
        self.n += 1
        return Tl(self.st.enter_context(self.nc.sbuf_tensor(name or f"sb{self.n}", list(shape), dt)))

    def ps(self, shape, dt, name=None):
        self.n += 1
        return Tl(self.st.enter_context(self.nc.psum_tensor(name or f"ps{self.n}", list(shape), dt)))

    def rot(self, n, shape, dt, psum=False):
        return Rot([(self.ps if psum else self.sb)(shape, dt) for _ in range(n)])


class Rot:
    def __init__(self, tiles):
        self.tiles = tiles
        self.i = 0

    def next(self):
        t = self.tiles[self.i % len(self.tiles)]
        self.i += 1
        return t


def blocks_of(Tc):
    bl = [(0, NMETA)]
    t = NMETA
    while t < Tc:
        n = min(512, Tc - t)
        bl.append((t, n))
        t += n
    return bl


def dram_in(nc, name, shape, dt):
    return nc.dram_tensor(name, list(shape), dt, kind="ExternalInput")


def dram_out(nc, name, shape, dt):
    return nc.dram_tensor(name, list(shape), dt, kind="ExternalOutput")


def load_const(C, dram, shape, dt=F32, eng="sp"):
    t = C.sb(shape, dt)
    C.P.op(eng, lambda e: e.dma_start(out=t[:], in_=dram.ap()), writes=[t.r], dma=f"c{C.n}")
    return t


def make_ones(C, dt):
    t = C.sb([128, 128], dt)
    C.P.op("dve", lambda e: e.memset(t[:], 1.0), writes=[t.r])
    return t


def wload(C, wrot, w_ap, k0, kc, c0, ncols, key):
    wb = wrot.next()
    src = w_ap[k0 * 128:(k0 + kc) * 128, c0:c0 + ncols].rearrange("(k p) n -> p k n", p=128)
    slot = (wrot.i - 1) % len(wrot.tiles)
    C.P.op("pool", lambda e: e.dma_start(out=wb[:, 0:kc, 0:ncols], in_=src), writes=[wb.r], dma=f"{key}{slot}")
    return wb


def rstd_from_sum(C, ps_stat, rs, T, n):
    P = C.P
    P.op("dve", lambda e: e.tensor_scalar(out=rs[:, :T], in0=ps_stat[:, :T], scalar1=1.0 / n, scalar2=EPS,
                                          op0=ALU.mult, op1=ALU.add), reads=[ps_stat.r], writes=[rs.r])
    P.op("act", lambda e: e.activation(out=rs[:, :T], in_=rs[:, :T], func=AF.Sqrt), reads=[rs.r], writes=[rs.r])
    P.op("dve", lambda e: e.reciprocal(out=rs[:, :T], in_=rs[:, :T]), reads=[rs.r], writes=[rs.r])


def sumsq_chunks(C, src_fn, nk, T, ones, sqrot, ps_stat):
    P = C.P
    for k in range(nk):
        ap, res = src_fn(k)
        sq = sqrot.next()
        P.op("act", lambda e, ap=ap, sq=sq: e.activation(out=sq[:, :T], in_=ap, func=AF.Square),
             reads=[res], writes=[sq.r])
        P.op("pe", lambda e, sq=sq, k=k: e.matmul(ps_stat[:, :T], lhsT=ones[:], rhs=sq[:, :T],
                                                  start=(k == 0), stop=(k == nk - 1)),
             reads=[sq.r, ones.r], writes=[ps_stat.r])


def linear_group(C, wb, kc, nm, rhs_fn, rhs_res, T, psrot, evac, mrows=128):
    P = C.P
    for mi in range(nm):
        ps = psrot.next()
        for k in range(kc):
            P.op("pe", lambda e, ps=ps, k=k, mi=mi: e.matmul(ps[0:mrows, :T], lhsT=wb[:, k, mi * mrows:(mi + 1) * mrows],
                                                             rhs=rhs_fn(k), start=(k == 0), stop=(k == kc - 1)),
                 reads=[wb.r] + rhs_res, writes=[ps.r], inc=(k == kc - 1))
        evac(mi, ps)


_ev_flip = [0]


def copy_evac(C, out_ap, out_res, ps, T, rows=128):
    _ev_flip[0] ^= 1
    if _ev_flip[0]:
        C.P.op("act", lambda e: e.activation(out=out_ap, in_=ps[0:rows, :T], func=AF.Copy), reads=[ps.r], writes=[out_res])
    else:
        C.P.op("dve", lambda e: e.tensor_copy(out=out_ap, in_=ps[0:rows, :T]), reads=[ps.r], writes=[out_res])


def build_A(cfg):
    Tc = cfg.Tc
    nc = bass.Bass("TRN2", target_bir_lowering=False)
    xT = dram_in(nc, "xT", [D, Tc], F32).ap()
    g1_d = dram_in(nc, "g1", [128, KD], F32)
    w_main = dram_in(nc, "w_main", [D, 8192], F32).ap()
    w_small = dram_in(nc, "w_small", [D, 72], F32).ap()
    gcq_d = dram_in(nc, "gcq", [128, 12], F32)
    gckv_d = dram_in(nc, "gckv", [128, 4], F32)
    w_uq = dram_in(nc, "w_uq", [1536, 3072], F32).ap()
    w_ukv = dram_in(nc, "w_ukv", [512, 4096], F32).ap()
    cos_d = dram_in(nc, "cos4", [128, Tc], F32).ap()
    sin_d = dram_in(nc, "sin4", [128, Tc], F32).ap()
    o_qk = dram_out(nc, "o_qk", [2048, Tc], F32).ap()
    o_mv = dram_out(nc, "o_mv", [2048, Tc], BF16).ap()
    o_mo = dram_out(nc, "o_mo", [2048, Tc], F32).ap()
    o_g = dram_out(nc, "o_g", [8, Tc], F32).ap()
    o_q = dram_out(nc, "o_q", [3072, Tc], BF16).ap()
    o_kn = dram_out(nc, "o_kn", [2048, Tc], BF16).ap()
    o_v = dram_out(nc, "o_v", [2048, Tc], BF16).ap()
    o_kr = dram_out(nc, "o_kr", [64, Tc], BF16).ap()

    C = Ctx(nc)
    P = C.P
    with C.st:
        g1 = load_const(C, g1_d, [128, KD])
        gcq = load_const(C, gcq_d, [128, 12])
        gckv = load_const(C, gckv_d, [128, 4])
        ones = make_ones(C, F32)
        xrot = C.rot(2, [128, 512], F32)
        sqrot = C.rot(2, [128, 512], F32)
        rs = C.sb([128, 512], F32)
        uT = C.sb([128, KD, 512], BF16)
        wrot = C.rot(2, [128, KD, 512], BF16)
        wsm = C.sb([128, KD, 72], BF16)
        cq = C.sb([128, 12, 512], F32)
        ckv = C.sb([128, 4, 512], F32)
        cqb = C.sb([128, 12, 512], BF16)
        ckvb = C.sb([128, 4, 512], BF16)
        stf = C.rot(1, [128, 4, 512], F32)
        stb = C.rot(2, [128, 4, 512], BF16)
        cosT = C.sb([128, 512], F32)
        sinT = C.sb([128, 512], F32)
        x1s = C.sb([128, 4, 512], F32)
        tmp = C.rot(4, [128, 512], F32)
        krb = C.sb([32, 2, 512], BF16)
        gst = C.sb([8, 512], F32)
        ps_stat = C.ps([128, 512], F32)
        psrot = C.rot(4, [128, 512], F32, psum=True)
        ps_a = C.ps([128, 512], F32)
        ps_b = C.ps([128, 512], F32)
        r_out = Res()

        P.op("pool", lambda e: e.dma_start(out=wsm[:], in_=w_small.rearrange("(k p) n -> p k n", p=128)),
             writes=[wsm.r], dma="wsm")

        def do_block(t0, T):
            P.op("sp", lambda e, t0=t0, T=T: e.dma_start(out=cosT[:, :T], in_=cos_d[:, t0:t0 + T]), writes=[cosT.r], dma="cos")
            P.op("sp", lambda e, t0=t0, T=T: e.dma_start(out=sinT[:, :T], in_=sin_d[:, t0:t0 + T]), writes=[sinT.r], dma="sin")
            xts = []

            def xsrc(k, t0=t0, T=T):
                xk = xrot.next()
                slot = (xrot.i - 1) % 2
                P.op("sp", lambda e: e.dma_start(out=xk[:, :T], in_=xT[k * 128:(k + 1) * 128, t0:t0 + T]),
                     writes=[xk.r], dma=f"x{slot}")
                return xk[:, :T], xk.r

            sumsq_chunks(C, xsrc, KD, T, ones, sqrot, ps_stat)
            rstd_from_sum(C, ps_stat, rs, T, D)
            for k in range(KD):
                xk = xrot.next()
                slot = (xrot.i - 1) % 2
                P.op("sp", lambda e, xk=xk, k=k: e.dma_start(out=xk[:, :T], in_=xT[k * 128:(k + 1) * 128, t0:t0 + T]),
                     writes=[xk.r], dma=f"x{slot}")
                P.op("dve", lambda e, xk=xk, k=k: e.scalar_tensor_tensor(out=uT[:, k, :T], in0=xk[:, :T], scalar=g1[:, k:k + 1],
                                                                         in1=rs[:, :T], op0=ALU.mult, op1=ALU.mult),
                     reads=[xk.r, g1.r, rs.r], writes=[uT.r])
            for cg in range(16):
                wb = wload(C, wrot, w_main, 0, KD, cg * 512, 512, "wA")
                if cg < 4 or 8 <= cg < 12:
                    st_t = stf.next()
                    dst = (o_qk, cg * 512) if cg < 4 else (o_mo, (cg - 8) * 512)
                elif cg < 8:
                    st_t = stb.next()
                    dst = (o_mv, (cg - 4) * 512)
                else:
                    st_t = None
                    dst = None

                def evac(mi, ps, cg=cg, st_t=st_t):
                    if st_t is not None:
                        copy_evac(C, st_t[:, mi, :T], st_t.r, ps, T)
                    elif cg < 15:
                        copy_evac(C, cq[:, (cg - 12) * 4 + mi, :T], cq.r, ps, T)
                    else:
                        copy_evac(C, ckv[:, mi, :T], ckv.r, ps, T)

                linear_group(C, wb, KD, 4, lambda k: uT[:, k, :T], [uT.r], T, psrot, evac)
                if st_t is not None:
                    dd, r0 = dst
                    P.op("sp", lambda e, st_t=st_t, dd=dd, r0=r0: e.dma_start(
                        out=dd[r0:r0 + 512, t0:t0 + T].rearrange("(c p) t -> p c t", p=128), in_=st_t[:, :, :T]),
                        reads=[st_t.r], writes=[r_out], dma="outA")
            for k in range(KD):
                P.op("pe", lambda e, k=k: e.matmul(ps_stat[0:8, :T], lhsT=wsm[:, k, 0:8], rhs=uT[:, k, :T],
                                                   start=(k == 0), stop=(k == KD - 1)),
                     reads=[wsm.r, uT.r], writes=[ps_stat.r], inc=(k == KD - 1))
            P.op("act", lambda e: e.activation(out=gst[:, :T], in_=ps_stat[0:8, :T], func=AF.Copy), reads=[ps_stat.r], writes=[gst.r])
            P.op("sp", lambda e: e.dma_start(out=o_g[:, t0:t0 + T], in_=gst[:, :T]), reads=[gst.r], writes=[r_out], dma="outA")
            for k in range(KD):
                P.op("pe", lambda e, k=k: e.matmul(ps_a[0:32, :T], lhsT=wsm[:, k, 8:40], rhs=uT[:, k, :T],
                                                   start=(k == 0), stop=(k == KD - 1)),
                     reads=[wsm.r, uT.r], writes=[ps_a.r], inc=(k == KD - 1))
            for k in range(KD):
                P.op("pe", lambda e, k=k: e.matmul(ps_b[0:32, :T], lhsT=wsm[:, k, 40:72], rhs=uT[:, k, :T],
                                                   start=(k == 0), stop=(k == KD - 1)),
                     reads=[wsm.r, uT.r], writes=[ps_b.r], inc=(k == KD - 1))

            def rope(x1ap, x1res, x2ap, x2res, o1, o2, ores, rows):
                ta, tb, tc_, td = tmp.next(), tmp.next(), tmp.next(), tmp.next()
                P.op("dve", lambda e: e.tensor_tensor(out=ta[0:rows, :T], in0=x1ap, in1=cosT[0:rows, :T], op=ALU.mult),
                     reads=[x1res, cosT.r], writes=[ta.r])
                P.op("dve", lambda e: e.tensor_tensor(out=tb[0:rows, :T], in0=x2ap, in1=sinT[0:rows, :T], op=ALU.mult),
                     reads=[x2res, sinT.r], writes=[tb.r])
                P.op("dve", lambda e: e.tensor_tensor(out=o1, in0=ta[0:rows, :T], in1=tb[0:rows, :T], op=ALU.subtract),
                     reads=[ta.r, tb.r], writes=[ores])
                P.op("dve", lambda e: e.tensor_tensor(out=tc_[0:rows, :T], in0=x1ap, in1=sinT[0:rows, :T], op=ALU.mult),
                     reads=[x1res, sinT.r], writes=[tc_.r])
                P.op("dve", lambda e: e.tensor_tensor(out=td[0:rows, :T], in0=x2ap, in1=cosT[0:rows, :T], op=ALU.mult),
                     reads=[x2res, cosT.r], writes=[td.r])
                P.op("dve", lambda e: e.tensor_tensor(out=o2, in0=tc_[0:rows, :T], in1=td[0:rows, :T], op=ALU.add),
                     reads=[tc_.r, td.r], writes=[ores])

            rope(ps_a[0:32, :T], ps_a.r, ps_b[0:32, :T], ps_b.r, krb[:, 0, :T], krb[:, 1, :T], krb.r, 32)
            P.op("sp", lambda e: e.dma_start(out=o_kr[:, t0:t0 + T].rearrange("(c p) t -> p c t", p=32), in_=krb[:, :, :T]),
                 reads=[krb.r], writes=[r_out], dma="outA")
            sumsq_chunks(C, lambda k: (cq[:, k, :T], cq.r), 12, T, ones, sqrot, ps_stat)
            rstd_from_sum(C, ps_stat, rs, T, 1536)
            for k in range(12):
                P.op("dve", lambda e, k=k: e.scalar_tensor_tensor(out=cqb[:, k, :T], in0=cq[:, k, :T], scalar=gcq[:, k:k + 1],
                                                                  in1=rs[:, :T], op0=ALU.mult, op1=ALU.mult),
                     reads=[cq.r, gcq.r, rs.r], writes=[cqb.r])
            for cg in range(6):
                wb = wload(C, wrot, w_uq, 0, 12, cg * 512, 512, "wA")
                st_t = stb.next()

                def evac(mi, ps, cg=cg, st_t=st_t):
                    if cg < 4:
                        copy_evac(C, st_t[:, mi, :T], st_t.r, ps, T)
                    elif cg == 4:
                        copy_evac(C, x1s[:, mi, :T], x1s.r, ps, T)
                    else:
                        rope(x1s[:, mi, :T], x1s.r, ps[:, :T], ps.r, stx1[:, mi, :T], st_t[:, mi, :T], st_t.r, 128)

                if cg == 5:
                    stx1 = stb.next()
                linear_group(C, wb, 12, 4, lambda k: cqb[:, k, :T], [cqb.r], T, psrot, evac)
                if cg < 4:
                    P.op("sp", lambda e, st_t=st_t, cg=cg: e.dma_start(
                        out=o_q[cg * 512:(cg + 1) * 512, t0:t0 + T].rearrange("(c p) t -> p c t", p=128), in_=st_t[:, :, :T]),
                        reads=[st_t.r], writes=[r_out], dma="outA")
                elif cg == 5:
                    P.op("sp", lambda e, stx1=stx1: e.dma_start(
                        out=o_q[2048:2560, t0:t0 + T].rearrange("(c p) t -> p c t", p=128), in_=stx1[:, :, :T]),
                        reads=[st_t.r, stx1.r], writes=[r_out, stx1.r], dma="outA")
                    P.op("sp", lambda e, st_t=st_t: e.dma_start(
                        out=o_q[2560:3072, t0:t0 + T].rearrange("(c p) t -> p c t", p=128), in_=st_t[:, :, :T]),
                        reads=[st_t.r], writes=[r_out], dma="outA")
            sumsq_chunks(C, lambda k: (ckv[:, k, :T], ckv.r), 4, T, ones, sqrot, ps_stat)
            rstd_from_sum(C, ps_stat, rs, T, 512)
            for k in range(4):
                P.op("dve", lambda e, k=k: e.scalar_tensor_tensor(out=ckvb[:, k, :T], in0=ckv[:, k, :T], scalar=gckv[:, k:k + 1],
                                                                  in1=rs[:, :T], op0=ALU.mult, op1=ALU.mult),
                     reads=[ckv.r, gckv.r, rs.r], writes=[ckvb.r])
            for cg in range(8):
                wb = wload(C, wrot, w_ukv, 0, 4, cg * 512, 512, "wA")
                st_t = stb.next()
                linear_group(C, wb, 4, 4, lambda k: ckvb[:, k, :T], [ckvb.r], T, psrot,
                             lambda mi, ps, st_t=st_t: copy_evac(C, st_t[:, mi, :T], st_t.r, ps, T))
                dd, r0 = (o_kn, cg * 512) if cg < 4 else (o_v, (cg - 4) * 512)
                P.op("sp", lambda e, st_t=st_t, dd=dd, r0=r0: e.dma_start(
                    out=dd[r0:r0 + 512, t0:t0 + T].rearrange("(c p) t -> p c t", p=128), in_=st_t[:, :, :T]),
                    reads=[st_t.r], writes=[r_out], dma="outA")
        for (t0, T) in blocks_of(Tc):
            do_block(t0, T)
        P.emit()
    return nc


def build_MT(cfg):
    Tt = cfg.Ttot
    nc = bass.Bass("TRN2", target_bir_lowering=False)
    qkT_h = dram_in(nc, "qkT", [512, Tt], F32)
    qkT = qkT_h.ap()
    convw_d = dram_in(nc, "convw", [128, 16], F32)
    vtok = dram_in(nc, "vtok", [Tt, 256], BF16).ap()
    gates_h = dram_in(nc, "gates", [2, Tt], F32)
    bg_d = dram_in(nc, "bg", [128, 2], F32)
    aq = dram_in(nc, "aq", [2, 192, Tt], BF16).ap()
    ak = dram_in(nc, "ak", [2, 192, Tt], BF16).ap()
    av = dram_in(nc, "av", [2, Tt, 128], BF16).ap()
    o_hm = dram_out(nc, "o_hm", [Tt, 256], F32).ap()
    o_ha = dram_out(nc, "o_ha", [256, Tt], BF16).ap()

    NCH = (Tt - NMETA) // 128
    SBK = 1024
    C = Ctx(nc)
    P = C.P
    with C.st:
        convw = load_const(C, convw_d, [128, 16])
        bg = load_const(C, bg_d, [128, 2])
        nbf = C.sb([128, 1], F32)
        P.op("dve", lambda e: e.tensor_scalar(out=nbf[:], in0=bg[:, 1:2], scalar1=-1.0, scalar2=None, op0=ALU.mult),
             reads=[bg.r], writes=[nbf.r])
        ident = C.sb([128, 128], BF16)
        P.op("pool", lambda e: e.memset(ident[:], 1.0), writes=[ident.r])
        P.op("pool", lambda e: e.affine_select(out=ident[:], in_=ident[:], pattern=[[1, 128]], compare_op=ALU.is_equal,
                                               fill=0.0, base=0, channel_multiplier=-1), reads=[ident.r], writes=[ident.r])
        mask = C.sb([128, 128], F32)
        P.op("pool", lambda e: e.memset(mask[:], 1.0), writes=[mask.r])
        P.op("pool", lambda e: e.affine_select(out=mask[:], in_=mask[:], pattern=[[1, 128]], compare_op=ALU.is_ge,
                                               fill=0.0, base=0, channel_multiplier=-1), reads=[mask.r], writes=[mask.r])
        onesS = C.sb([128, SBK], F32)
        P.op("dve", lambda e: e.memset(onesS[:], 1.0), writes=[onesS.r])
        raw = C.sb([128, 4, 3 + SBK], F32)
        cv = C.sb([128, 4, SBK], F32)
        qp = C.sb([128, 4, SBK], BF16)
        gi = C.sb([128, SBK], F32)
        gf = C.sb([128, SBK], F32)
        bcum = C.sb([128, SBK], F32)
        wq = C.sb([128, SBK], F32)
        wk = C.sb([128, SBK], F32)
        gG = C.sb([128, SBK // 128], F32)
        vaug = C.sb([128, SBK // 128, 257], BF16)
        Cst = C.sb([128, 2, 257], F32)
        Cb = C.sb([128, 2, 257], BF16)
        ktok = C.rot(2, [128, 256], BF16)
        sT = C.rot(2, [128, 128], BF16)
        hrot = C.rot(2, [128, 256], F32)
        den = C.rot(2, [128, 1], F32)
        ps_s = C.ps([128, 128], F32)
        ps_o = C.ps([128, 257], F32)
        ps_c = [C.ps([128, 257], F32), C.ps([128, 257], F32)]
        ps_t = C.ps([128, 256], BF16)
        r_out = Res()
        P.op("dve", lambda e: e.memset(Cst[:], 0.0), writes=[Cst.r])
        P.op("dve", lambda e: e.memset(Cb[:], 0.0), writes=[Cb.r])
        P.op("dve", lambda e: e.memset(vaug[:, :, 256:257], 1.0), writes=[vaug.r])

        sblocks = [(0, NMETA)]
        t = NMETA
        while t < Tt:
            n = min(SBK, Tt - t)
            sblocks.append((t, n))
            t += n
        def do_sb(t0, S):
            if t0 == 0:
                P.op("dve", lambda e: e.memset(raw[:, :, 0:3], 0.0), writes=[raw.r])
                P.op("sp", lambda e, S=S: e.dma_start(out=raw[:, :, 3:3 + S], in_=qkT[:, 0:S].rearrange("(c p) t -> p c t", p=128)),
                     writes=[raw.r], dma="raw")
            else:
                P.op("sp", lambda e, S=S, t0=t0: e.dma_start(out=raw[:, :, 0:3 + S],
                                                             in_=qkT[:, t0 - 3:t0 + S].rearrange("(c p) t -> p c t", p=128)),
                     writes=[raw.r], dma="raw")
            P.op("sp", lambda e, S=S, t0=t0: e.dma_start(out=gi[:, :S], in_=bass.AP(gates_h, t0, [[0, 128], [1, S]])),
                 writes=[gi.r], dma="gi")
            P.op("sp", lambda e, S=S, t0=t0: e.dma_start(out=gf[:, :S], in_=bass.AP(gates_h, Tt + t0, [[0, 128], [1, S]])),
                 writes=[gf.r], dma="gf")
            nch = max(1, S // 128)
            Lc = S if S < 128 else 128
            if S < 128:
                P.op("sp", lambda e, S=S, t0=t0: e.dma_start(out=vaug[0:S, 0, 0:256], in_=vtok[t0:t0 + S, :]), writes=[vaug.r], dma="v")
            else:
                P.op("sp", lambda e, S=S, t0=t0, nch=nch: e.dma_start(out=vaug[:, 0:nch, 0:256],
                                                                      in_=vtok[t0:t0 + S, :].rearrange("(j p) d -> p j d", p=128)),
                     writes=[vaug.r], dma="v")
            for c in range(4):
                P.op("dve", lambda e, c=c, S=S: e.tensor_scalar(out=cv[:, c, :S], in0=raw[:, c, 3:3 + S],
                                                                scalar1=convw[:, c * 4 + 3:c * 4 + 4], scalar2=None, op0=ALU.mult),
                     reads=[raw.r, convw.r], writes=[cv.r])
                for j in range(3):
                    P.op("dve", lambda e, c=c, j=j, S=S: e.scalar_tensor_tensor(out=cv[:, c, :S], in0=raw[:, c, j:j + S],
                                                                                scalar=convw[:, c * 4 + j:c * 4 + j + 1],
                                                                                in1=cv[:, c, :S], op0=ALU.mult, op1=ALU.add),
                         reads=[raw.r, convw.r, cv.r], writes=[cv.r])
            P.op("act", lambda e, S=S: e.activation(out=cv[:, :, :S], in_=cv[:, :, :S], func=AF.Silu), reads=[cv.r], writes=[cv.r])
            P.op("act", lambda e, S=S: e.activation(out=gf[:, :S], in_=gf[:, :S], func=AF.Exp, bias=nbf[:, 0:1], scale=-1.0),
                 reads=[gf.r, nbf.r], writes=[gf.r])
            P.op("act", lambda e, S=S: e.activation(out=gf[:, :S], in_=gf[:, :S], func=AF.Ln, bias=1.0, scale=1.0),
                 reads=[gf.r], writes=[gf.r])
            for j in range(nch):
                P.op("dve", lambda e, j=j, Lc=Lc: e.tensor_tensor_scan(out=bcum[:, j * Lc:(j + 1) * Lc], data0=onesS[:, 0:Lc],
                                                                       data1=gf[:, j * Lc:(j + 1) * Lc], initial=0.0,
                                                                       op0=ALU.mult, op1=ALU.subtract),
                     reads=[gf.r, onesS.r], writes=[bcum.r])
            P.op("act", lambda e, S=S: e.activation(out=wq[:, :S], in_=bcum[:, :S], func=AF.Exp, bias=-math.log(16.0), scale=1.0),
                 reads=[bcum.r], writes=[wq.r])
            P.op("dve", lambda e, S=S: e.scalar_tensor_tensor(out=wk[:, :S], in0=gi[:, :S], scalar=bg[:, 0:1], in1=bcum[:, :S],
                                                              op0=ALU.add, op1=ALU.subtract),
                 reads=[gi.r, bg.r, bcum.r], writes=[wk.r])
            P.op("act", lambda e, S=S: e.activation(out=wk[:, :S], in_=wk[:, :S], func=AF.Exp), reads=[wk.r], writes=[wk.r])
            P.op("act", lambda e, S=S, nch=nch, Lc=Lc: e.activation(
                out=gG[:, 0:nch], in_=bcum[:, 0:nch * Lc].rearrange("p (j l) -> p j l", l=Lc)[:, :, Lc - 1], func=AF.Exp),
                reads=[bcum.r], writes=[gG.r])
            for c in range(4):
                wsrc = wq if c < 2 else wk
                P.op("dve", lambda e, c=c, S=S, wsrc=wsrc: e.tensor_tensor(out=qp[:, c, :S], in0=cv[:, c, :S], in1=wsrc[:, :S], op=ALU.mult),
                     reads=[cv.r, wsrc.r], writes=[qp.r])
            def do_chunk(j):
                c0 = j * Lc
                tg = t0 + c0
                kt = ktok.next()
                for c in range(2):
                    P.op("pe", lambda e, c=c, c0=c0, Lc=Lc: e.transpose(ps_t[0:Lc, c * 128:(c + 1) * 128], qp[:, 2 + c, c0:c0 + Lc], ident[:]),
                         reads=[qp.r, ident.r], writes=[ps_t.r], inc=(c == 1))
                P.op("act", lambda e, kt=kt, Lc=Lc: e.activation(out=kt[0:Lc, :], in_=ps_t[0:Lc, :], func=AF.Copy), reads=[ps_t.r], writes=[kt.r])
                for c in range(2):
                    P.op("pe", lambda e, c=c, c0=c0, Lc=Lc: e.matmul(ps_s[0:Lc, 0:Lc], lhsT=qp[:, 2 + c, c0:c0 + Lc], rhs=qp[:, c, c0:c0 + Lc],
                                                                     start=(c == 0), stop=(c == 1)),
                         reads=[qp.r], writes=[ps_s.r], inc=(c == 1))
                st_ = sT.next()
                P.op("dve", lambda e, st_=st_, Lc=Lc: e.tensor_tensor(out=st_[0:Lc, 0:Lc], in0=ps_s[0:Lc, 0:Lc], in1=mask[0:Lc, 0:Lc], op=ALU.mult),
                     reads=[ps_s.r, mask.r], writes=[st_.r])
                P.op("pe", lambda e, st_=st_, j=j, Lc=Lc: e.matmul(ps_o[0:Lc, :], lhsT=st_[0:Lc, 0:Lc], rhs=vaug[0:Lc, j, :], start=True, stop=False),
                     reads=[st_.r, vaug.r], writes=[ps_o.r], inc=False)
                for c in range(2):
                    P.op("pe", lambda e, c=c, c0=c0, Lc=Lc: e.matmul(ps_o[0:Lc, :], lhsT=qp[:, c, c0:c0 + Lc], rhs=Cb[:, c, :], start=False, stop=(c == 1)),
                         reads=[qp.r, Cb.r], writes=[ps_o.r], inc=(c == 1))
                dn = den.next()
                P.op("act", lambda e, dn=dn, Lc=Lc: e.activation(out=dn[0:Lc, :], in_=ps_o[0:Lc, 256:257], func=AF.Abs),
                     reads=[ps_o.r], writes=[dn.r])
                P.op("dve", lambda e, dn=dn, Lc=Lc: e.tensor_scalar(out=dn[0:Lc, :], in0=dn[0:Lc, :], scalar1=1.0, scalar2=None, op0=ALU.max),
                     reads=[dn.r], writes=[dn.r])
                P.op("dve", lambda e, dn=dn, Lc=Lc: e.reciprocal(out=dn[0:Lc, :], in_=dn[0:Lc, :]), reads=[dn.r], writes=[dn.r])
                ht = hrot.next()
                P.op("act", lambda e, ht=ht, dn=dn, Lc=Lc: e.activation(out=ht[0:Lc, :], in_=ps_o[0:Lc, 0:256], func=AF.Copy, scale=dn[0:Lc, 0:1]),
                     reads=[ps_o.r, dn.r], writes=[ht.r])
                P.op("sp", lambda e, ht=ht, tg=tg, Lc=Lc: e.dma_start(out=o_hm[tg:tg + Lc, :], in_=ht[0:Lc, :]),
                     reads=[ht.r], writes=[r_out], dma="ohm")
                for c in range(2):
                    P.op("pe", lambda e, c=c, kt=kt, j=j, Lc=Lc: e.matmul(ps_c[c][:, :], lhsT=kt[0:Lc, c * 128:(c + 1) * 128], rhs=vaug[0:Lc, j, :],
                                                                           start=True, stop=True),
                         reads=[kt.r, vaug.r], writes=[ps_c[c].r])
                    P.op("dve", lambda e, c=c: e.tensor_tensor(out=Cst[:, c, :], in0=ps_c[c][:, :], in1=Cst[:, c, :], op=ALU.add),
                         reads=[ps_c[c].r, Cst.r], writes=[Cst.r])
                    P.op("dve", lambda e, c=c, j=j: e.tensor_scalar(out=Cst[:, c, :], in0=Cst[:, c, :], scalar1=gG[:, j:j + 1], scalar2=None, op0=ALU.mult),
                         reads=[Cst.r, gG.r], writes=[Cst.r])
                    P.op("act", lambda e, c=c: e.activation(out=Cb[:, c, :], in_=Cst[:, c, :], func=AF.Copy), reads=[Cst.r], writes=[Cb.r])
            for j in range(nch):
                do_chunk(j)
        for (t0, S) in sblocks:
            do_sb(t0, S)
        P.emit()

    C = Ctx(nc)
    P = C.P
    NKB = NCH
    scale = 192.0 ** -0.5
    with C.st:
        onesb = make_ones(C, BF16)
        masks = []
        for j in range(4):
            m = C.sb([128, 512], BF16)
            P.op("pool", lambda e, m=m: e.memset(m[:], 1.0), writes=[m.r])
            P.op("pool", lambda e, m=m, j=j: e.affine_select(out=m[:], in_=m[:], pattern=[[1, 512]], compare_op=ALU.is_ge,
                                                             fill=0.0, base=-128 * j, channel_multiplier=-1), reads=[m.r], writes=[m.r])
            masks.append(m)
        kn = C.sb([128, Tt], BF16)
        kr = C.sb([64, Tt], BF16)
        vt = C.sb([128, NKB + 1, 128], BF16)
        qn = C.rot(2, [128, 512], BF16)
        qr = C.rot(2, [64, 512], BF16)
        pT = C.rot(3, [128, 512], BF16)
        rden = C.sb([128, 512], F32)
        ob = C.rot(2, [128, 512], BF16)
        ps_s = C.rot(3, [128, 512], F32, psum=True)
        ps_o = C.ps([128, 512], F32)
        ps_d = C.ps([128, 512], F32)
        r_out = Res()
        qblocks = [(0, NMETA)]
        t = NMETA
        while t < Tt:
            n = min(512, Tt - t)
            qblocks.append((t, n))
            t += n
        def do_head(h):
            P.op("sp", lambda e, h=h: e.dma_start(out=kn[:], in_=ak[h, 0:128, :]), writes=[kn.r], dma="kn")
            P.op("sp", lambda e, h=h: e.dma_start(out=kr[:], in_=ak[h, 128:192, :]), writes=[kr.r], dma="kr")
            P.op("sp", lambda e, h=h: e.dma_start(out=vt[0:NMETA, 0, :], in_=av[h, 0:NMETA, :]), writes=[vt.r], dma="vt")
            for j0 in range(0, NKB, 8):
                jn = min(8, NKB - j0)
                P.op("sp", lambda e, h=h, j0=j0, jn=jn: e.dma_start(
                    out=vt[:, 1 + j0:1 + j0 + jn, :],
                    in_=av[h, NMETA + j0 * 128:NMETA + (j0 + jn) * 128, :].rearrange("(j p) d -> p j d", p=128)),
                    writes=[vt.r], dma="vt")
            def do_qb(q0, Q):
                qn_t, qr_t = qn.next(), qr.next()
                slot = (qn.i - 1) % 2
                P.op("act", lambda e, h=h, q0=q0, Q=Q, qn_t=qn_t: e.dma_start(out=qn_t[:, :Q], in_=aq[h, 0:128, q0:q0 + Q]),
                     writes=[qn_t.r], dma=f"qn{slot}")
                P.op("act", lambda e, h=h, q0=q0, Q=Q, qr_t=qr_t: e.dma_start(out=qr_t[:, :Q], in_=aq[h, 128:192, q0:q0 + Q]),
                     writes=[qr_t.r], dma=f"qr{slot}")
                kbs = [(0, NMETA, 0, (masks[0] if q0 == 0 else None))]
                if q0 > 0:
                    nfull = (q0 - NMETA) // 128
                    for jb in range(nfull):
                        kbs.append((NMETA + jb * 128, 128, 1 + jb, None))
                    for jd in range((Q + 127) // 128):
                        kbs.append((q0 + jd * 128, 128, 1 + nfull + jd, masks[jd]))
                nkb = len(kbs)
                def do_kb(bi, k0, K, vs, msk):
                    ps = ps_s.next()
                    P.op("pe", lambda e, ps=ps, k0=k0, K=K, Q=Q, qn_t=qn_t: e.matmul(ps[0:K, :Q], lhsT=kn[:, k0:k0 + K], rhs=qn_t[:, :Q], start=True, stop=False),
                         reads=[kn.r, qn_t.r], writes=[ps.r], inc=False)
                    P.op("pe", lambda e, ps=ps, k0=k0, K=K, Q=Q, qr_t=qr_t: e.matmul(ps[0:K, :Q], lhsT=kr[:, k0:k0 + K], rhs=qr_t[:, :Q], start=False, stop=True),
                         reads=[kr.r, qr_t.r], writes=[ps.r])
                    p_t = pT.next()
                    P.op("act", lambda e, ps=ps, p_t=p_t, K=K, Q=Q: e.activation(out=p_t[0:K, :Q], in_=ps[0:K, :Q], func=AF.Exp, scale=scale),
                         reads=[ps.r], writes=[p_t.r])
                    if msk is not None:
                        P.op("dve", lambda e, p_t=p_t, msk=msk, K=K, Q=Q: e.tensor_tensor(out=p_t[0:K, :Q], in0=p_t[0:K, :Q], in1=msk[0:K, :Q], op=ALU.mult),
                             reads=[p_t.r, msk.r], writes=[p_t.r])
                    P.op("pe", lambda e, p_t=p_t, K=K, Q=Q, vs=vs, bi=bi, nkb=nkb: e.matmul(ps_o[:, :Q], lhsT=vt[0:K, vs, :], rhs=p_t[0:K, :Q],
                                                                                         start=(bi == 0), stop=(bi == nkb - 1)),
                         reads=[vt.r, p_t.r], writes=[ps_o.r], inc=False)
                    P.op("pe", lambda e, p_t=p_t, K=K, Q=Q, bi=bi, nkb=nkb: e.matmul(ps_d[:, :Q], lhsT=onesb[0:K, :], rhs=p_t[0:K, :Q],
                                                                                  start=(bi == 0), stop=(bi == nkb - 1)),
                         reads=[onesb.r, p_t.r], writes=[ps_d.r])
                for bi, (k0, K, vs, msk) in enumerate(kbs):
                    do_kb(bi, k0, K, vs, msk)
                P.op("dve", lambda e, Q=Q: e.reciprocal(out=rden[:, :Q], in_=ps_d[:, :Q]), reads=[ps_d.r], writes=[rden.r])
                o_t = ob.next()
                P.op("dve", lambda e, Q=Q, o_t=o_t: e.tensor_tensor(out=o_t[:, :Q], in0=ps_o[:, :Q], in1=rden[:, :Q], op=ALU.mult),
                     reads=[ps_o.r, rden.r], writes=[o_t.r])
                P.op("sp", lambda e, h=h, q0=q0, Q=Q, o_t=o_t: e.dma_start(out=o_ha[h * 128:(h + 1) * 128, q0:q0 + Q], in_=o_t[:, :Q]),
                     reads=[o_t.r], writes=[r_out], dma="oha")
            for (q0, Q) in qblocks:
                do_qb(q0, Q)
        for h in range(2):
            do_head(h)
        P.emit()
    return nc


def build_C(cfg):
    Tc, DFF = cfg.Tc, cfg.DFF
    KF = DFF // 128
    nc = bass.Bass("TRN2", target_bir_lowering=False)
    xT = dram_in(nc, "xT", [D, Tc], F32).ap()
    hmT = dram_in(nc, "hmT", [2048, Tc], F32).ap()
    moT = dram_in(nc, "moT", [2048, Tc], F32).ap()
    haT = dram_in(nc, "haT", [2048, Tc], BF16).ap()
    gmn_d = dram_in(nc, "gmn", [128, 16], F32)
    w_out = dram_in(nc, "w_out", [D, D], F32).ap()
    gpost_d = dram_in(nc, "gpost", [128, KD], F32)
    gfpre_d = dram_in(nc, "gfpre", [128, KD], F32)
    w_gu = dram_in(nc, "w_gu", [D, 2 * DFF], F32).ap()
    w_down = dram_in(nc, "w_down", [DFF, D], F32).ap()
    gfpost_d = dram_in(nc, "gfpost", [128, KD], F32)
    oT = dram_out(nc, "oT", [D, Tc], F32).ap()
    x1T = nc.dram_tensor("x1T", [D, Tc], F32, kind="Internal").ap()

    C = Ctx(nc)
    P = C.P
    with C.st:
        gmn = load_const(C, gmn_d, [128, 16])
        gpost = load_const(C, gpost_d, [128, KD])
        gfpre = load_const(C, gfpre_d, [128, KD])
        gfpost = load_const(C, gfpost_d, [128, KD])
        ones = make_ones(C, F32)
        inrot = C.rot(2, [128, 512], F32)
        in2rot = C.rot(2, [128, 512], F32)
        sqrot = C.rot(2, [128, 512], F32)
        rs = C.sb([128, 512], F32)
        hmb = C.sb([128, 4, 512], F32)
        aT = C.sb([128, KD, 512], BF16)
        yacc = C.sb([128, KD, 512], F32)
        wrot = C.rot(2, [128, KD, 512], BF16)
        hT = C.rot(1, [128, 8, 512], BF16)
        gt = C.rot(5, [128, 512], F32)
        ps_stat = C.ps([128, 512], F32)
        psrot = C.rot(5, [128, 512], F32, psum=True)
        r_x1 = Res()
        r_out = Res()

        def dma_in(rot, key, src_ap, T, eng="sp"):
            tl = rot.next()
            slot = (rot.i - 1) % len(rot.tiles)
            P.op(eng, lambda e: e.dma_start(out=tl[:, :T], in_=src_ap), writes=[tl.r], dma=f"{key}{slot}")
            return tl

        def do_block(t0, T):
            for hd in range(4):
                def src(k, hd=hd):
                    r0 = (hd * 4 + k) * 128
                    P.op("sp", lambda e: e.dma_start(out=hmb[:, k, :T], in_=hmT[r0:r0 + 128, t0:t0 + T]), writes=[hmb.r], dma="hmb")
                    return hmb[:, k, :T], hmb.r
                sumsq_chunks(C, src, 4, T, ones, sqrot, ps_stat)
                rstd_from_sum(C, ps_stat, rs, T, 512)
                for k in range(4):
                    r0 = (hd * 4 + k) * 128
                    mo = dma_in(inrot, "mo", moT[r0:r0 + 128, t0:t0 + T], T)
                    P.op("act", lambda e, mo=mo: e.activation(out=mo[:, :T], in_=mo[:, :T], func=AF.Sigmoid), reads=[mo.r], writes=[mo.r])
                    tt = in2rot.next()
                    P.op("dve", lambda e, tt=tt, k=k, hd=hd: e.scalar_tensor_tensor(out=tt[:, :T], in0=hmb[:, k, :T], scalar=gmn[:, hd * 4 + k:hd * 4 + k + 1],
                                                                                    in1=rs[:, :T], op0=ALU.mult, op1=ALU.mult),
                         reads=[hmb.r, gmn.r, rs.r], writes=[tt.r])
                    P.op("dve", lambda e, tt=tt, mo=mo, k=k, hd=hd: e.tensor_tensor(out=aT[:, hd * 4 + k, :T], in0=tt[:, :T], in1=mo[:, :T], op=ALU.mult),
                         reads=[tt.r, mo.r], writes=[aT.r])
            P.op("sp", lambda e: e.dma_start(out=aT[:, 16:32, :T], in_=haT[:, t0:t0 + T].rearrange("(c p) t -> p c t", p=128)),
                 writes=[aT.r], dma="haT")
            for cg in range(8):
                wb = wload(C, wrot, w_out, 0, KD, cg * 512, 512, "wC")
                linear_group(C, wb, KD, 4, lambda k: aT[:, k, :T], [aT.r], T, psrot,
                             lambda mi, ps, cg=cg: copy_evac(C, yacc[:, cg * 4 + mi, :T], yacc.r, ps, T))
            sumsq_chunks(C, lambda k: (yacc[:, k, :T], yacc.r), KD, T, ones, sqrot, ps_stat)
            rstd_from_sum(C, ps_stat, rs, T, D)
            for k in range(KD):
                xk = dma_in(inrot, "mo", xT[k * 128:(k + 1) * 128, t0:t0 + T], T)
                P.op("dve", lambda e, k=k: e.scalar_tensor_tensor(out=yacc[:, k, :T], in0=yacc[:, k, :T], scalar=gpost[:, k:k + 1],
                                                                  in1=rs[:, :T], op0=ALU.mult, op1=ALU.mult),
                     reads=[yacc.r, gpost.r, rs.r], writes=[yacc.r])
                P.op("dve", lambda e, k=k, xk=xk: e.tensor_tensor(out=yacc[:, k, :T], in0=yacc[:, k, :T], in1=xk[:, :T], op=ALU.add),
                     reads=[yacc.r, xk.r], writes=[yacc.r])
            P.op("sp", lambda e: e.dma_start(out=x1T[:, t0:t0 + T].rearrange("(c p) t -> p c t", p=128), in_=yacc[:, :, :T]),
                 reads=[yacc.r], writes=[r_x1], dma="x1w")
            sumsq_chunks(C, lambda k: (yacc[:, k, :T], yacc.r), KD, T, ones, sqrot, ps_stat)
            rstd_from_sum(C, ps_stat, rs, T, D)
            for k in range(KD):
                P.op("dve", lambda e, k=k: e.scalar_tensor_tensor(out=aT[:, k, :T], in0=yacc[:, k, :T], scalar=gfpre[:, k:k + 1],
                                                                  in1=rs[:, :T], op0=ALU.mult, op1=ALU.mult),
                     reads=[yacc.r, gfpre.r, rs.r], writes=[aT.r])
            ngr = (KF + 7) // 8
            for g in range(ngr):
                f0 = g * 8
                nf = min(8, KF - f0)
                h_t = hT.next()
                gtiles = {}
                for half in range((nf + 3) // 4):
                    nm = min(4, nf - half * 4)
                    wg = wload(C, wrot, w_gu, 0, KD, (f0 + half * 4) * 128, nm * 128, "wC")

                    def evac_g(mi, ps, half=half):
                        g_t = gt.next()
                        P.op("act", lambda e, g_t=g_t, ps=ps: e.activation(out=g_t[:, :T], in_=ps[:, :T], func=AF.Silu), reads=[ps.r], writes=[g_t.r])
                        gtiles[half * 4 + mi] = g_t
                    linear_group(C, wg, KD, nm, lambda k: aT[:, k, :T], [aT.r], T, psrot, evac_g)
                    wu = wload(C, wrot, w_gu, 0, KD, DFF + (f0 + half * 4) * 128, nm * 128, "wC")

                    def evac_u(mi, ps, half=half, h_t=h_t):
                        g_t = gtiles[half * 4 + mi]
                        P.op("dve", lambda e, g_t=g_t, ps=ps, idx=half * 4 + mi: e.tensor_tensor(out=h_t[:, idx, :T], in0=ps[:, :T], in1=g_t[:, :T], op=ALU.mult),
                             reads=[ps.r, g_t.r], writes=[h_t.r])
                    linear_group(C, wu, KD, nm, lambda k: aT[:, k, :T], [aT.r], T, psrot, evac_u)
                for cg in range(8):
                    wd = wload(C, wrot, w_down, f0, nf, cg * 512, 512, "wC")

                    def evac_d(mi, ps, cg=cg, g=g):
                        m = cg * 4 + mi
                        if g == 0:
                            copy_evac(C, yacc[:, m, :T], yacc.r, ps, T)
                        else:
                            P.op("dve", lambda e: e.tensor_tensor(out=yacc[:, m, :T], in0=ps[:, :T], in1=yacc[:, m, :T], op=ALU.add),
                                 reads=[ps.r, yacc.r], writes=[yacc.r])
                    linear_group(C, wd, nf, 4, lambda k, h_t=h_t: h_t[:, k, :T], [h_t.r], T, psrot, evac_d)
            sumsq_chunks(C, lambda k: (yacc[:, k, :T], yacc.r), KD, T, ones, sqrot, ps_stat)
            rstd_from_sum(C, ps_stat, rs, T, D)
            for k in range(KD):
                xk = in2rot.next()
                slot = (in2rot.i - 1) % 2
                P.op("sp", lambda e, xk=xk, k=k: e.dma_start(out=xk[:, :T], in_=x1T[k * 128:(k + 1) * 128, t0:t0 + T]),
                     reads=[r_x1], writes=[xk.r], dma=f"x1r{slot}")
                P.op("dve", lambda e, k=k: e.scalar_tensor_tensor(out=yacc[:, k, :T], in0=yacc[:, k, :T], scalar=gfpost[:, k:k + 1],
                                                                  in1=rs[:, :T], op0=ALU.mult, op1=ALU.mult),
                     reads=[yacc.r, gfpost.r, rs.r], writes=[yacc.r])
                P.op("dve", lambda e, k=k, xk=xk: e.tensor_tensor(out=yacc[:, k, :T], in0=yacc[:, k, :T], in1=xk[:, :T], op=ALU.add),
                     reads=[yacc.r, xk.r], writes=[yacc.r])
            P.op("sp", lambda e: e.dma_start(out=oT[:, t0:t0 + T].rearrange("(c p) t -> p c t", p=128), in_=yacc[:, :, :T]),
                 reads=[yacc.r], writes=[r_out], dma="outC")
        for (t0, T) in blocks_of(Tc):
            do_block(t0, T)
        P.emit()
    return nc


_PROGS = {}


def _prog(name, cfg):
    key = (name, cfg.SEQ, cfg.DFF)
    if key not in _PROGS:
        _PROGS[key] = {"A": build_A, "MT": build_MT, "C": build_C}[name](cfg)
    return _PROGS[key]


def _pk(g, n):
    return np.ascontiguousarray(np.asarray(g, np.float32).reshape(n, 128).T)


def _run(nc, in_maps):
    res = run_bass_kernel_spmd(nc, in_maps, core_ids=list(range(NCORES)))
    return res.results


def forward(cfg, x, meta, g_mix_pre, w_in, conv_w, b_gates, g_mnorm, g_cq, w_uq, g_ckv, w_ukv,
            w_out, g_mix_post, g_ffn_pre, w_gu, w_down, g_ffn_post, dbg=None, stop=None):
    SEQ, Tc, Tt = cfg.SEQ, cfg.Tc, cfg.Ttot
    per = SEQ // NCORES
    x = np.asarray(x, np.float32)[0]
    meta = np.asarray(meta, np.float32)
    xTs = [np.ascontiguousarray(np.concatenate([meta, x[c * per:(c + 1) * per]], axis=0).T) for c in range(NCORES)]
    inv_freq = (10000.0 ** (-np.arange(32, dtype=np.float32) / np.float32(32))).astype(np.float32)
    cs = []
    for c in range(NCORES):
        pos = np.concatenate([np.arange(NMETA), NMETA + c * per + np.arange(per)]).astype(np.float32)
        ang = (pos[:, None] * inv_freq[None, :]).astype(np.float32)
        cs.append((np.ascontiguousarray(np.tile(np.cos(ang).T, (4, 1)).astype(np.float32)),
                   np.ascontiguousarray(np.tile(np.sin(ang).T, (4, 1)).astype(np.float32))))

    def seq(parts, axis):
        sl = [slice(None)] * parts[0].ndim
        out = []
        for c, p in enumerate(parts):
            s = list(sl)
            s[axis] = slice(0 if c == 0 else NMETA, None)
            out.append(p[tuple(s)])
        return np.concatenate(out, axis=axis)

    def shard(full, axis):
        out = []
        for c in range(NCORES):
            idx = np.concatenate([np.arange(NMETA), NMETA + c * per + np.arange(per)])
            out.append(np.ascontiguousarray(np.take(full, idx, axis=axis)))
        return out

    for l in range(cfg.DEPTH):
        wi = np.asarray(w_in[l], np.float32)
        w_main = np.ascontiguousarray(np.concatenate([wi[:, 0:6144], wi[:, 6152:8200]], axis=1))
        w_small = np.ascontiguousarray(np.concatenate([wi[:, 6144:6152], wi[:, 8200:8264]], axis=1))
        wq = np.asarray(w_uq[l], np.float32).reshape(1536, 16, 192)
        w_uq_p = np.ascontiguousarray(np.concatenate([wq[:, :, 0:128].reshape(1536, 2048), wq[:, :, 128:160].reshape(1536, 512),
                                                      wq[:, :, 160:192].reshape(1536, 512)], axis=1))
        wkv = np.asarray(w_ukv[l], np.float32).reshape(512, 16, 256)
        w_ukv_p = np.ascontiguousarray(np.concatenate([wkv[:, :, 0:128].reshape(512, 2048), wkv[:, :, 128:256].reshape(512, 2048)], axis=1))
        ncA = _prog("A", cfg)
        inA = [{"xT": xTs[c], "g1": _pk(g_mix_pre[l], KD), "w_main": w_main, "w_small": w_small,
                "gcq": _pk(g_cq[l], 12), "gckv": _pk(g_ckv[l], 4), "w_uq": w_uq_p, "w_ukv": w_ukv_p,
                "cos4": cs[c][0], "sin4": cs[c][1]} for c in range(NCORES)]
        rA = _run(ncA, inA)
        del inA, w_main, w_small
        qk = seq([r["o_qk"] for r in rA], 1)
        mv = seq([r["o_mv"] for r in rA], 1)
        gts = seq([r["o_g"] for r in rA], 1)
        qa = seq([r["o_q"] for r in rA], 1)
        kna = seq([r["o_kn"] for r in rA], 1)
        va = seq([r["o_v"] for r in rA], 1)
        kra = seq([r["o_kr"] for r in rA], 1)
        moTs = [r["o_mo"] for r in rA]
        del rA
        if dbg is not None and l == 0:
            dbg.update(qk=qk, mv=mv, gts=gts, qa=qa, kna=kna, va=va, kra=kra, mo=seq(moTs, 1))
            if stop == "A":
                return None
        cw = np.asarray(conv_w[l], np.float32)
        bgl = np.asarray(b_gates[l], np.float32)
        inM = []
        for c in range(NCORES):
            hd, half = c // 2, c % 2
            qkT = np.ascontiguousarray(np.concatenate([qk[hd * 256:(hd + 1) * 256], qk[1024 + hd * 256:1024 + (hd + 1) * 256]], axis=0))
            chans = np.concatenate([hd * 256 + np.arange(256), 1024 + hd * 256 + np.arange(256)])
            cwh = cw[:, chans]
            convw = np.ascontiguousarray(cwh.reshape(4, 4, 128).transpose(2, 1, 0).reshape(128, 16))
            vtok = np.ascontiguousarray(mv[hd * 512 + half * 256: hd * 512 + (half + 1) * 256].T)
            gates = np.ascontiguousarray(np.stack([gts[hd], gts[4 + hd]], axis=0))
            bg = np.ascontiguousarray(np.tile(np.array([[bgl[hd], bgl[4 + hd]]], np.float32), (128, 1)))
            aq = np.empty((2, 192, Tt), NPBF)
            ak_ = np.empty((2, 192, Tt), NPBF)
            av_ = np.empty((2, Tt, 128), NPBF)
            for i in range(2):
                ah = 2 * c + i
                aq[i, 0:128] = qa[ah * 128:(ah + 1) * 128]
                aq[i, 128:160] = qa[2048 + ah * 32:2048 + (ah + 1) * 32]
                aq[i, 160:192] = qa[2560 + ah * 32:2560 + (ah + 1) * 32]
                ak_[i, 0:128] = kna[ah * 128:(ah + 1) * 128]
                ak_[i, 128:192] = kra
                av_[i] = va[ah * 128:(ah + 1) * 128].T
            inM.append({"qkT": qkT, "convw": convw, "vtok": vtok, "gates": gates, "bg": bg, "aq": aq, "ak": ak_, "av": av_})
        del qk, mv, qa, kna, va
        rM = _run(_prog("MT", cfg), inM)
        del inM
        hm_full = np.concatenate([rM[c]["o_hm"] for c in range(NCORES)], axis=1)
        ha_full = np.concatenate([rM[c]["o_ha"] for c in range(NCORES)], axis=0)
        del rM
        if dbg is not None and l == 0:
            dbg.update(hm=hm_full, ha=ha_full)
            if stop == "MT":
                return None
        hmTs = shard(np.ascontiguousarray(hm_full.T), 1)
        haTs = shard(ha_full, 1)
        inC = [{"xT": xTs[c], "hmT": hmTs[c], "moT": moTs[c], "haT": haTs[c], "gmn": _pk(g_mnorm[l], 16),
                "w_out": np.asarray(w_out[l], np.float32), "gpost": _pk(g_mix_post[l], KD), "gfpre": _pk(g_ffn_pre[l], KD),
                "w_gu": np.asarray(w_gu[l], np.float32), "w_down": np.asarray(w_down[l], np.float32),
                "gfpost": _pk(g_ffn_post[l], KD)} for c in range(NCORES)]
        rC = _run(_prog("C", cfg), inC)
        del inC, hmTs, haTs, moTs
        xTs = [rC[c]["oT"] for c in range(NCORES)]
        del rC
    out = np.concatenate([xTs[c][:, NMETA:].T for c in range(NCORES)], axis=0)
    return np.ascontiguousarray(out[None].astype(np.float32))


def kernel(**inputs):
    return forward(Cfg(), **inputs)
```
